# Optimizing a Trainium2 kernel written in Bass

```python
import math
import jax, jax.numpy as jnp
from jax import lax
import numpy as np

D_MODEL = 2048
BATCH = 4
SEQ = 2048
DEPTH = 2

HEAD_DIM = 128
ATTN_SLOTS = 4
DILATED_PATTERNS = ((128, 1), (512, 4), (2048, 16))
N_GROUPS = len(DILATED_PATTERNS)
N_ATTN_HEADS = N_GROUPS * ATTN_SLOTS
ATTN_WIDTH = N_ATTN_HEADS * HEAD_DIM
ATTN_OUT = ATTN_SLOTS * HEAD_DIM
ROPE_THETA = 500000.0
ROPE_DIM = HEAD_DIM // 4
CONV_WIDTH = D_MODEL // 2
CONV_KERNEL = 31
GMLP_WIDTH = D_MODEL // 2
GMLP_CHUNK = 128
GMLP_GROUPS = 8
GMLP_GROUP_CH = GMLP_WIDTH // GMLP_GROUPS
FFN_HIDDEN = -(-8 * D_MODEL // (3 * 256)) * 256
DEEPNORM_ALPHA = (2 * DEPTH) ** 0.25
DEEPNORM_BETA = (8 * DEPTH) ** -0.25
LN_EPS = 1e-5
NEG_INF = -1e30
IN_SIZES = (ATTN_WIDTH, ATTN_WIDTH, ATTN_WIDTH, 2 * CONV_WIDTH, 2 * GMLP_WIDTH, 3 * D_MODEL)
IN_WIDTH = sum(IN_SIZES)
IN_SPLITS = tuple(sum(IN_SIZES[:i + 1]) for i in range(len(IN_SIZES) - 1))

kernel_name = "hybrid_dilated_conv_sgu_gated_deepnorm"


def layer_norm(x, g, b):
    xf = x.astype(jnp.float32)
    mu = xf.mean(-1, keepdims=True)
    var = jnp.square(xf - mu).mean(-1, keepdims=True)
    return ((xf - mu) * lax.rsqrt(var + LN_EPS) * g.astype(jnp.float32)
            + b.astype(jnp.float32)).astype(x.dtype)


def partial_rotary(x, pos):
    half = ROPE_DIM // 2
    inv_freq = ROPE_THETA ** (-jnp.arange(half, dtype=jnp.float32) / half)
    ang = pos.astype(jnp.float32)[:, None] * inv_freq[None, :]
    cos = jnp.cos(ang)[None, :, None, :]
    sin = jnp.sin(ang)[None, :, None, :]
    xr = x[..., :ROPE_DIM].astype(jnp.float32)
    x1, x2 = xr[..., :half], xr[..., half:]
    rot = jnp.concatenate([x1 * cos - x2 * sin, x2 * cos + x1 * sin], -1).astype(x.dtype)
    return jnp.concatenate([rot, x[..., ROPE_DIM:]], -1)


def dilated_window_attention(q, k, v, dilation, band):
    B, S, H, hd = q.shape
    L = S // dilation
    N = B * dilation

    def to_residue(t):
        return t.reshape(B, L, dilation, H, hd).transpose(0, 2, 1, 3, 4).reshape(N, L, H, hd)

    qr, kr, vr = to_residue(q), to_residue(k), to_residue(v)
    nb = -(-L // band)
    Lp = nb * band
    pad = Lp - L
    qr = jnp.pad(qr, ((0, 0), (0, pad), (0, 0), (0, 0)))
    kr = jnp.pad(kr, ((0, 0), (band, pad), (0, 0), (0, 0)))
    vr = jnp.pad(vr, ((0, 0), (band, pad), (0, 0), (0, 0)))
    qb = qr.reshape(N, nb, band, H, hd)
    kb = kr.reshape(N, nb + 1, band, H, hd)
    vb = vr.reshape(N, nb + 1, band, H, hd)
    kw = jnp.concatenate([kb[:, :-1], kb[:, 1:]], axis=2)
    vw = jnp.concatenate([vb[:, :-1], vb[:, 1:]], axis=2)
    scores = jnp.einsum('nbqhd,nbkhd->nbhqk', qb, kw).astype(jnp.float32) * (hd ** -0.5)
    i = jnp.arange(band)[None, :, None]
    j = jnp.arange(2 * band)[None, None, :]
    key_pos = jnp.arange(nb)[:, None, None] * band - band + j
    valid = (j >= i) & (j <= i + band) & (key_pos >= 0)
    scores = jnp.where(valid[None, :, None], scores, NEG_INF)
    m = scores.max(-1, keepdims=True)
    p = jnp.exp(scores - m)
    denom = p.sum(-1, keepdims=True)
    out = jnp.einsum('nbhqk,nbkhd->nbqhd', (p / denom).astype(v.dtype), vw)
    lse = (m + jnp.log(denom))[..., 0]
    out = (out.reshape(N, Lp, H, hd)[:, :L]
           .reshape(B, dilation, L, H, hd).transpose(0, 2, 1, 3, 4).reshape(B, S, H, hd))
    lse = (lse.transpose(0, 1, 3, 2).reshape(N, Lp, H)[:, :L]
           .reshape(B, dilation, L, H).transpose(0, 2, 1, 3).reshape(B, S, H))
    return out, lse


def conformer_conv(a, w_dw, b_dw, ln_g, ln_b, w_proj):
    val, gate = jnp.split(a, 2, -1)
    h = val * jax.nn.sigmoid(gate)
    C = h.shape[-1]
    h = jnp.pad(h, ((0, 0), (CONV_KERNEL - 1, 0), (0, 0)))
    h = lax.conv_general_dilated(h, w_dw[:, None, :].astype(h.dtype), window_strides=(1,),
                                 padding='VALID', dimension_numbers=('NWC', 'WIO', 'NWC'),
                                 feature_group_count=C) + b_dw
    h = jax.nn.silu(layer_norm(h, ln_g, ln_b))
    return h @ w_proj


def chunked_spatial_gating(z, ln_g, ln_b, w_s, b_s, w_proj):
    z = jax.nn.gelu(z, approximate=False)
    u, v = jnp.split(z, 2, -1)
    v = layer_norm(v, ln_g, ln_b)
    B, S, _ = v.shape
    nc = S // GMLP_CHUNK
    vc = v.reshape(B, nc, GMLP_CHUNK, GMLP_GROUPS, GMLP_GROUP_CH)
    causal = jnp.tril(jnp.ones((GMLP_CHUNK, GMLP_CHUNK), dtype=bool))
    w_m = jnp.where(causal[None], w_s, 0).astype(v.dtype)
    s = jnp.einsum('gts,bnsgc->bntgc', w_m, vc) + b_s.T[None, None, :, :, None]
    return (u * s.reshape(B, S, GMLP_WIDTH)) @ w_proj


def hybrid_mixer(x, pos, w_in, w_attn_proj, conv_dw, conv_dw_b, conv_ln_g, conv_ln_b,
                 w_conv_proj, gmlp_ln_g, gmlp_ln_b, w_spatial, b_spatial, w_gmlp_proj, w_out):
    B, S, _ = x.shape
    proj = x @ w_in
    q, k, v, conv_in, gmlp_in, gates = jnp.split(proj, IN_SPLITS, axis=-1)
    q = partial_rotary(q.reshape(B, S, N_ATTN_HEADS, HEAD_DIM), pos)
    k = partial_rotary(k.reshape(B, S, N_ATTN_HEADS, HEAD_DIM), pos)
    v = v.reshape(B, S, N_ATTN_HEADS, HEAD_DIM)
    outs, lses = [], []
    for g, (window, dilation) in enumerate(DILATED_PATTERNS):
        sl = slice(g * ATTN_SLOTS, (g + 1) * ATTN_SLOTS)
        o, l = dilated_window_attention(q[:, :, sl], k[:, :, sl], v[:, :, sl],
                                        dilation, window // dilation)
        outs.append(o)
        lses.append(l)
    wts = jax.nn.softmax(jnp.stack(lses, 0), axis=0)
    attn = jnp.sum(wts[..., None].astype(v.dtype) * jnp.stack(outs, 0), axis=0)
    y_attn = attn.reshape(B, S, ATTN_OUT) @ w_attn_proj
    y_conv = conformer_conv(conv_in, conv_dw, conv_dw_b, conv_ln_g, conv_ln_b, w_conv_proj)
    y_gmlp = chunked_spatial_gating(gmlp_in, gmlp_ln_g, gmlp_ln_b, w_spatial, b_spatial, w_gmlp_proj)
    g_a, g_c, g_m = jnp.split(jax.nn.sigmoid(gates), 3, axis=-1)
    merged = g_a * y_attn + g_c * y_conv + g_m * y_gmlp
    return merged @ w_out


def swiglu(x, w_gate, w_up, w_down):
    return (jax.nn.silu(x @ w_gate) * (x @ w_up)) @ w_down


def setup_inputs(seed: int = 0) -> dict:
    key = jax.random.key(seed)
    ks = jax.random.split(key, 22)

    def nrm(k, shape, scale):
        return jax.random.normal(k, shape, dtype=jnp.float32) * scale

    L = DEPTH
    return {
        "x": nrm(ks[0], (BATCH, SEQ, D_MODEL), 1.0),
        "w_in": nrm(ks[1], (L, D_MODEL, IN_WIDTH), D_MODEL ** -0.5),
        "w_attn_proj": nrm(ks[2], (L, ATTN_OUT, D_MODEL), ATTN_OUT ** -0.5),
        "conv_dw": nrm(ks[3], (L, CONV_KERNEL, CONV_WIDTH), CONV_KERNEL ** -0.5),
        "conv_dw_b": nrm(ks[4], (L, CONV_WIDTH), 0.02),
        "conv_ln_g": 1.0 + nrm(ks[5], (L, CONV_WIDTH), 0.02),
        "conv_ln_b": nrm(ks[6], (L, CONV_WIDTH), 0.02),
        "w_conv_proj": nrm(ks[7], (L, CONV_WIDTH, D_MODEL), CONV_WIDTH ** -0.5),
        "gmlp_ln_g": 1.0 + nrm(ks[8], (L, GMLP_WIDTH), 0.02),
        "gmlp_ln_b": nrm(ks[9], (L, GMLP_WIDTH), 0.02),
        "w_spatial": nrm(ks[10], (L, GMLP_GROUPS, GMLP_CHUNK, GMLP_CHUNK), GMLP_CHUNK ** -0.5),
        "b_spatial": 1.0 + nrm(ks[11], (L, GMLP_GROUPS, GMLP_CHUNK), 0.02),
        "w_gmlp_proj": nrm(ks[12], (L, GMLP_WIDTH, D_MODEL), GMLP_WIDTH ** -0.5),
        "w_out": nrm(ks[13], (L, D_MODEL, D_MODEL), D_MODEL ** -0.5 * DEEPNORM_BETA),
        "ln1_g": 1.0 + nrm(ks[14], (L, D_MODEL), 0.02),
        "ln1_b": nrm(ks[15], (L, D_MODEL), 0.02),
        "w_ffn_gate": nrm(ks[16], (L, D_MODEL, FFN_HIDDEN), D_MODEL ** -0.5),
        "w_ffn_up": nrm(ks[17], (L, D_MODEL, FFN_HIDDEN), D_MODEL ** -0.5),
        "w_ffn_down": nrm(ks[18], (L, FFN_HIDDEN, D_MODEL), FFN_HIDDEN ** -0.5 * DEEPNORM_BETA),
        "ln2_g": 1.0 + nrm(ks[19], (L, D_MODEL), 0.02),
        "ln2_b": nrm(ks[20], (L, D_MODEL), 0.02),
    }


def reference(x, w_in, w_attn_proj, conv_dw, conv_dw_b, conv_ln_g, conv_ln_b, w_conv_proj,
              gmlp_ln_g, gmlp_ln_b, w_spatial, b_spatial, w_gmlp_proj, w_out, ln1_g, ln1_b,
              w_ffn_gate, w_ffn_up, w_ffn_down, ln2_g, ln2_b):
    pos = jnp.arange(x.shape[1], dtype=jnp.int32)
    for l in range(DEPTH):
        y = hybrid_mixer(x, pos, w_in[l], w_attn_proj[l], conv_dw[l], conv_dw_b[l],
                         conv_ln_g[l], conv_ln_b[l], w_conv_proj[l], gmlp_ln_g[l], gmlp_ln_b[l],
                         w_spatial[l], b_spatial[l], w_gmlp_proj[l], w_out[l])
        x = layer_norm(DEEPNORM_ALPHA * x + y, ln1_g[l], ln1_b[l])
        f = swiglu(x, w_ffn_gate[l], w_ffn_up[l], w_ffn_down[l])
        x = layer_norm(DEEPNORM_ALPHA * x + f, ln2_g[l], ln2_b[l])
    return x
```

```python
import numpy as np
import concourse.bass as bass
import concourse.mybir as mybir
from concourse.bass_utils import run_bass_kernel_spmd
from contextlib import ExitStack

F32 = mybir.dt.float32
BF16 = mybir.dt.bfloat16
AF = mybir.ActivationFunctionType
ALU = mybir.AluOpType

D = 2048
S = 2048
NB = 4
TOWN = 1024
DIL = (1, 4, 16)
FH = 5632
ALPHA = float(4.0 ** 0.25)
EPS = 1e-5
NEG = -30000.0
NSLOT = 4
WSLOT = 4096
ROPE_THETA = 500000.0
HPERM = np.concatenate([np.arange(0, 16), np.arange(32, 48), np.arange(16, 32), np.arange(48, 128)])


class PB:
    def __init__(self, nc, es):
        self.nc, self.es = nc, es
        self.E = dict(pe=nc.tensor, act=nc.scalar, dve=nc.vector, pool=nc.gpsimd, sp=nc.sync)
        self.semh, self.cnt, self.known = {}, {}, {}
        for e in self.E:
            self.semh[e] = es.enter_context(nc.semaphore("s_" + e))
            self.cnt[e] = 0
            self.known[e] = {}
        self.lanes = {}
        self.lane_i = {}
        for q, n in (("pool", 6), ("sp", 8)):
            self.lanes[q] = []
            self.lane_i[q] = 0
            for i in range(n):
                nm = "l_%s%d" % (q, i)
                self.semh[nm] = es.enter_context(nc.semaphore(nm))
                self.cnt[nm] = 0
                self.lanes[q].append(nm)
        self.lastw = {}
        self.rd = {}
        self.bank_i = 0
        self.w_i = 0
        self.dumps = []

    def _wait(self, e, ev):
        s, v = ev
        if v <= 0 or self.known[e].get(s, 0) >= v:
            return
        self.E[e].wait_ge(self.semh[s], v)
        self.known[e][s] = v

    def _deps(self, e, reads, writes):
        for k in reads:
            if k in self.lastw:
                self._wait(e, self.lastw[k])
        for k in writes:
            if k in self.lastw:
                self._wait(e, self.lastw[k])
            for s, v in self.rd.get(k, {}).items():
                self._wait(e, (s, v))

    def _done(self, ev, reads, writes):
        s, v = ev
        for k in reads:
            d = self.rd.setdefault(k, {})
            if d.get(s, 0) < v:
                d[s] = v
        for k in writes:
            self.lastw[k] = ev
            self.rd[k] = {}

    def op(self, e, fn, reads=(), writes=()):
        self._deps(e, reads, writes)
        ins = fn(self.E[e])
        self.cnt[e] += 1
        ins.then_inc(self.semh[e], 1)
        if e == "pe":
            self.known[e][e] = self.cnt[e]
        self._done((e, self.cnt[e]), reads, writes)

    def dma(self, q, out, in_, reads=(), writes=(), **kw):
        lane = self.lanes[q][self.lane_i[q] % len(self.lanes[q])]
        self.lane_i[q] += 1
        self._wait(q, (lane, self.cnt[lane]))
        self._deps(q, reads, writes)
        self.E[q].dma_start(out=out, in_=in_, **kw).then_inc(self.semh[lane], 16)
        self.cnt[lane] += 16
        self._done((lane, self.cnt[lane]), reads, writes)

    def bank(self):
        b = self.bank_i % 8
        self.bank_i += 1
        return b

    def bank2(self):
        if self.bank_i % 2:
            self.bank_i += 1
        b = self.bank_i % 8
        self.bank_i += 2
        return b

    def finish(self):
        for e in self.E:
            for s in self.semh:
                if s != e:
                    self._wait(e, (s, self.cnt[s]))


def mmg(pb, out, pairs, reads, wkey):
    n = len(pairs)

    def fn(pe):
        ins = None
        for i, (a, b) in enumerate(pairs):
            ins = pe.matmul(out, lhsT=a, rhs=b, start=(i == 0), stop=(i == n - 1))
        return ins
    pb.op("pe", fn, reads=reads, writes=[wkey])


def build_program(layers, debug=()):
    nc = bass.Bass("TRN2", target_bir_lowering=False)
    nl = len(layers)
    last = layers[-1] == 1

    def din(name, shape, dt=F32):
        return nc.dram_tensor(name, list(shape), dt, kind="ExternalInput").ap()

    def dscr(name, shape, dt):
        return nc.dram_tensor(name, list(shape), dt, kind="Internal").ap()

    xin = din("xin", [128, 16, 2048])
    W = {}
    for nm, nt, fr in (("wq", 6, 4096), ("wk", 6, 4096), ("wv", 6, 4096), ("wconv", 8, 4096), ("wgu", 4, 4096),
                       ("wgv", 4, 4096), ("wmA", 16, 2560), ("wmB", 48, 2048), ("wout", 8, 4096),
                       ("wffn", 44, 4096), ("wdn", 32, 2816)):
        W[nm] = din(nm, [nl * nt, 128, fr])
    pconv_d = din("pconv", [nl, 128, 272])
    gb_d = din("gbb", [nl, 128, 2048])
    wst_d = din("wst", [nl, 128, 1024])
    par_d = din("par", [nl, 128, 1024])
    ln_d = din("lnp", [nl, 128, 64])
    ident_d = din("ident", [128, 128])
    m01_d = din("m01", [128, 128])
    masks_d = din("masks", [128, 512])
    rope_d = din("rope", [128, 2, 2048])
    flag_d = din("flag", [128, 1])
    if last:
        outd = nc.dram_tensor("out", [128, 16, 1024], F32, kind="ExternalOutput").ap()
    else:
        outd = nc.dram_tensor("out", [128, 16, 2048], F32, kind="ExternalOutput").ap()
    x1d = dscr("x1s", [128, 16, 2048], F32) if nl == 2 else None
    kscr = [dscr("kscr%d" % i, [128, 12, 2048], BF16) for i in range(nl)]
    vscr = [dscr("vscr%d" % i, [2048, 1536], BF16) for i in range(nl)]
    qscr = dscr("qscr", [128, 12, 1024], BF16)
    xr = dscr("xres", [128, 16, 1024], F32)

    es = ExitStack()
    with es:
        pb = PB(nc, es)

        def sb(name, shape, dt):
            return es.enter_context(nc.sbuf_tensor(name, list(shape), dt))
        XT = sb("XT", [128, 16, 1024], BF16)
        WS = [sb("WS%d" % i, [128, WSLOT], BF16) for i in range(NSLOT)]
        AR = sb("AR", [128, 44 * 1024], BF16)
        TMP = [sb("TMP%d" % i, [128, 512], F32) for i in range(6)]
        ROPE = sb("ROPE", [128, 2, 1024], F32)
        GB = sb("GB", [128, 2048], F32)
        PCONV = sb("PCONV", [128, 272], F32)
        WST = sb("WSTf", [128, 1024], F32)
        WSM = sb("WSM", [128, 8, 128], BF16)
        PAR = sb("PAR", [128, 1024], F32)
        LNP = sb("LNP", [128, 64], F32)
        NM = sb("NM", [128, 4, 128], BF16)
        IDENT = sb("IDENT", [128, 128], BF16)
        M01 = sb("M01", [128, 128], F32)
        ONESB = sb("ONESB", [128, 128], BF16)
        ONESF = sb("ONESF", [128, 128], F32)
        FLAG = sb("FLAG", [128, 1], F32)
        EPSB = sb("EPSB", [128, 1], F32)
        HALO = sb("HALO", [128, 8, 32], F32)
        STATS = sb("STATS", [128, 2, 6], F32)
        MV = sb("MV", [128, 4], F32)
        PT = [sb("PT%d" % i, [128, 512], BF16) for i in range(2)]
        PS = es.enter_context(nc.psum_tensor("PS", [128, 8, 512], F32))

        def view(off_kb, dt, shape):
            n = int(np.prod(shape))
            nb = n * (4 if dt == F32 else 2)
            ap = AR[:, off_kb * 512: off_kb * 512 + nb // 2]
            if dt == F32:
                ap = ap.bitcast(F32)
            if len(shape) == 2:
                return ap.rearrange("p (a b) -> p a b", a=shape[0])
            if len(shape) == 3:
                return ap.rearrange("p (a b c) -> p a b c", a=shape[0], b=shape[1])
            return ap

        def ark(lo, hi):
            return [("ar", i) for i in range(lo // 2, (hi + 1) // 2)]

        CT = view(0, BF16, [8, 1024]); CTK = ark(0, 16)
        G = view(16, BF16, [8, 1024]); GK = ark(16, 32)
        AT = view(32, BF16, [4, 1024]); ATK = ark(32, 40)
        H = view(40, F32, [8, 1056]); HK = ark(40, 74)
        VG = view(40, F32, [8, 1024]); VGK = ark(40, 72)
        VN = view(72, BF16, [8, 1024]); VNK = ark(72, 88)
        MG = view(40, BF16, [16, 1024]); MGK = ark(40, 72)
        Z = view(0, F32, [16, 512]); ZK = ark(0, 32)
        HT = view(0, BF16, [44, 1024])
        Z2 = XT[:, :, :].rearrange("p a b -> p (a b)").bitcast(F32).rearrange("p (a b) -> p a b", a=16)
        KH = view(40, BF16, [2048]); KHK = ark(40, 44)
        VH = view(44, BF16, [16, 128]); VHK = ark(44, 48)
        QH = view(48, BF16, [1024]); QHK = ark(48, 50)
        UDA = view(50, F32, [2, 1024]); UDAK = ark(50, 58)
        QF = view(74, F32, [1024]); QFK = ark(74, 78)
        RA = view(78, F32, [1024]); RAK = ark(78, 82)
        RB = view(82, F32, [1024]); RBK = ark(82, 86)
        QS = view(86, BF16, [1024]); QSK = ark(86, 88)
        VS = sb("VS", [128, 8, 256], BF16)[:]; VSK = ["vs"]
        RD = view(58, F32, [1024]); RDK = ark(58, 62)
        XTK = [("xT", k) for k in range(16)]

        pb.dma("pool", IDENT[:], ident_d, writes=["ident"])
        pb.dma("pool", NM[:].rearrange("p a b -> p (a b)"), masks_d, writes=["nm"])
        pb.dma("sp", M01[:], m01_d, writes=["m01"])
        pb.dma("sp", FLAG[:], flag_d, writes=["flag"])
        pb.op("dve", lambda e: e.memset(ONESB[:], 1.0), writes=["onesb"])
        pb.op("dve", lambda e: e.memset(ONESF[:], 1.0), writes=["onesf"])
        pb.op("dve", lambda e: e.memset(EPSB[:], EPS), writes=["eps"])

        pb.op("dve", lambda e: e.memset(AR[:, 0:24576], 0.0), writes=ark(0, 48))
        for i_ in range(nl):
            pb.dma("sp", kscr[i_].rearrange("p a b -> p (a b)"), AR[:, 0:24576], reads=ark(0, 48),
                   writes=[("kscr", i_, h_) for h_ in range(12)])
            pb.dma("sp", vscr[i_].rearrange("(a p) c -> p a c", p=128), AR[:, 0:24576].rearrange("p (a c) -> p a c", a=16), reads=ark(0, 48),
                   writes=[("vscr", i_, h_) for h_ in range(12)])
        X1K = []

        def load_w(name, idx, n):
            s = pb.w_i % NSLOT
            pb.w_i += 1
            pb.dma("pool", WS[s][:, 0:n], W[name][idx], writes=[("w", s)], max_dma_last_dim=8192)
            return WS[s], ("w", s)

        def tmp(i):
            return TMP[i][:], ("tmp", i)

        def ln_fm(src, srck, nch, outfn):
            nfeat = float(nch * 128)
            bs_, bq = pb.bank(), pb.bank()
            sq, sqk = tmp(0)
            def fsum(pe):
                ins = None
                for c in range(nch):
                    ins = pe.matmul(PS[:, bs_, :], lhsT=ONESF[:], rhs=src(c), start=(c == 0), stop=(c == nch - 1))
                return ins
            pb.op("pe", fsum, reads=srck + ["onesf"], writes=[("ps", bs_)])
            for c in range(nch):
                pb.op("act", lambda e: e.activation(out=sq, in_=src(c), func=AF.Square), reads=srck, writes=[sqk])
                pb.op("pe", lambda e: e.matmul(PS[:, bq, :], lhsT=ONESF[:], rhs=sq, start=(c == 0), stop=(c == nch - 1)),
                      reads=[sqk, "onesf"], writes=[("ps", bq)])
            mean, meank = tmp(1)
            var, vark = tmp(2)
            rstd, rstdk = tmp(3)
            pb.op("act", lambda e: e.activation(out=mean, in_=PS[:, bs_, :], func=AF.Copy, scale=1.0 / nfeat),
                  reads=[("ps", bs_)], writes=[meank])
            pb.op("dve", lambda e: e.tensor_tensor(out=var, in0=mean, in1=mean, op=ALU.mult), reads=[meank], writes=[vark])
            pb.op("dve", lambda e: e.scalar_tensor_tensor(out=var, in0=PS[:, bq, :], scalar=1.0 / nfeat, in1=var,
                                                          op0=ALU.mult, op1=ALU.subtract),
                  reads=[("ps", bq), vark], writes=[vark])
            pb.op("act", lambda e: e.activation(out=var, in_=var, func=AF.Sqrt, bias=EPSB[:], scale=1.0),
                  reads=[vark, "eps"], writes=[vark])
            pb.op("dve", lambda e: e.reciprocal(out=rstd, in_=var), reads=[vark], writes=[rstdk])
            for c in range(nch):
                t, tk = tmp(4 + (c % 2))
                pb.op("dve", lambda e: e.tensor_tensor(out=t, in0=src(c), in1=mean, op=ALU.subtract),
                      reads=srck + [meank], writes=[tk])
                pb.op("dve", lambda e: e.tensor_tensor(out=t, in0=t, in1=rstd, op=ALU.mult), reads=[tk, rstdk], writes=[tk])
                outfn(c, t, tk)

        def run_pass(li, t0, kvonly, src, dst):
            l = li
            pb.dma("sp", PCONV[:], pconv_d[l], writes=["pconv"])
            pb.dma("sp", GB[:], gb_d[l], writes=["gb"])
            pb.dma("sp", WST[:], wst_d[l], writes=["wst"])
            pb.dma("sp", PAR[:], par_d[l], writes=["par"])
            pb.dma("sp", LNP[:], ln_d[l], writes=["lnp"])
            pb.dma("sp", ROPE[:], rope_d[:, :, t0:t0 + 1024], writes=["rope"])
            pb.dma("pool", XT[:], src[:, :, t0:t0 + 1024], reads=(X1K if li > 0 else []), writes=XTK, max_dma_last_dim=8192)

            if t0 == 0:
                pb.op("dve", lambda e: e.memset(HALO[:], 0.0), writes=["halo"])
            pb.op("dve", lambda e: e.tensor_copy(out=H[:, :, 0:32], in_=HALO[:]), reads=["halo"], writes=HK)
            for i in range(8):
                wt, wk = load_w("wconv", l * 8 + i, 4096)
                wv = wt[:, :].rearrange("p (k n) -> p k n", k=16)
                if kvonly:
                    spans = [(896, 128)]
                else:
                    spans = [(0, 512), (512, 512)]
                for (c0, n) in spans:
                    bv, bg = pb.bank(), pb.bank()
                    mmg(pb, PS[:, bv, 0:n], [(wv[:, kc, 0:128], XT[:, kc, c0:c0 + n]) for kc in range(16)],
                        reads=[wk] + XTK, wkey=("ps", bv))
                    mmg(pb, PS[:, bg, 0:n], [(wv[:, kc, 128:256], XT[:, kc, c0:c0 + n]) for kc in range(16)],
                        reads=[wk] + XTK, wkey=("ps", bg))
                    sg, sgk = tmp(i % 2)
                    pb.op("act", lambda e: e.activation(out=sg[:, 0:n], in_=PS[:, bg, 0:n], func=AF.Sigmoid),
                          reads=[("ps", bg)], writes=[sgk])
                    pb.op("dve", lambda e: e.tensor_tensor(out=H[:, i, 32 + c0:32 + c0 + n], in0=PS[:, bv, 0:n],
                                                           in1=sg[:, 0:n], op=ALU.mult),
                          reads=[("ps", bv), sgk], writes=HK)
            pb.op("dve", lambda e: e.tensor_scalar(out=HALO[:], in0=H[:, :, 1024:1056], scalar1=FLAG[:, 0:1], scalar2=None,
                                                   op0=ALU.mult), reads=HK + ["flag"], writes=["halo"])

            def qk_proj(name, is_q):
                for i in range(6):
                    wt, wk = load_w(name, l * 6 + i, 4096)
                    wv = wt[:, :].rearrange("p (k n) -> p k n", k=16)
                    for hh in range(2):
                        h = 2 * i + hh
                        d = DIL[h // 4]
                        b2 = pb.bank2()
                        for th in range(2):
                            mmg(pb, PS[:, b2 + th, :],
                                [(wv[:, kc, hh * 128:(hh + 1) * 128], XT[:, kc, th * 512:(th + 1) * 512]) for kc in range(16)],
                                reads=[wk] + XTK, wkey=("ps", b2 + th))
                        psk = [("ps", b2), ("ps", b2 + 1)]
                        pb.op("act", lambda e: e.activation(out=QF.rearrange("p (a b) -> p a b", a=2), in_=PS[:, b2:b2 + 2, :],
                                                            func=AF.Copy), reads=psk, writes=QFK)
                        C0, C32 = ROPE[0:16, 0, :], ROPE[32:48, 0, :]
                        S0, S32 = ROPE[0:16, 1, :], ROPE[32:48, 1, :]
                        pb.op("dve", lambda e: e.tensor_tensor(out=RA[0:16, :], in0=QF[0:16, :], in1=C0, op=ALU.mult),
                              reads=QFK + ["rope"], writes=RAK)
                        pb.op("dve", lambda e: e.tensor_tensor(out=RB[0:16, :], in0=QF[32:48, :], in1=S32, op=ALU.mult),
                              reads=QFK + ["rope"], writes=RBK)
                        pb.op("dve", lambda e: e.tensor_tensor(out=RA[32:48, :], in0=QF[32:48, :], in1=C32, op=ALU.mult),
                              reads=QFK + ["rope"], writes=RAK)
                        pb.op("dve", lambda e: e.tensor_tensor(out=RB[32:48, :], in0=QF[0:16, :], in1=S0, op=ALU.mult),
                              reads=QFK + ["rope"], writes=RBK)
                        pb.op("dve", lambda e: e.tensor_tensor(out=QF[0:16, :], in0=RA[0:16, :], in1=RB[0:16, :], op=ALU.subtract),
                              reads=RAK + RBK, writes=QFK)
                        pb.op("dve", lambda e: e.tensor_tensor(out=QF[32:48, :], in0=RA[32:48, :], in1=RB[32:48, :], op=ALU.add),
                              reads=RAK + RBK, writes=QFK)
                        if d == 1:
                            pb.op("act", lambda e: e.activation(out=QS, in_=QF, func=AF.Copy), reads=QFK, writes=QSK)
                        else:
                            pb.op("act", lambda e: e.activation(out=QS.rearrange("p (r m) -> p m r", r=d),
                                                                in_=QF.rearrange("p (m r) -> p m r", r=d), func=AF.Copy),
                                  reads=QFK, writes=QSK)
                        if is_q:
                            pb.dma("sp", qscr[:, h, :], QS, reads=QSK, writes=[("qscr", h)])
                        else:
                            L = 2048 // d
                            dstap = kscr[li][:, h, :].rearrange("p (r m) -> p r m", r=d)[:, :, t0 // d:t0 // d + 1024 // d]
                            pb.dma("sp", dstap, QS.rearrange("p (r m) -> p r m", r=d), reads=QSK, writes=[("kscr", li, h)])

            qk_proj("wk", False)
            for i in range(6):
                wt, wk = load_w("wv", l * 6 + i, 4096)
                wv = wt[:, :].rearrange("p (k n) -> p k n", k=16)
                for tb in range(8):
                    b = pb.bank()
                    mmg(pb, PS[:, b, 0:256], [(XT[:, kc, tb * 128:(tb + 1) * 128], wv[:, kc, :]) for kc in range(16)],
                        reads=[wk] + XTK, wkey=("ps", b))
                    pb.op("act", lambda e: e.activation(out=VS[:, tb, :], in_=PS[:, b, 0:256], func=AF.Copy),
                          reads=[("ps", b)], writes=VSK)
                pb.dma("sp", vscr[li][t0:t0 + 1024, i * 256:(i + 1) * 256].rearrange("(tb p) c -> p tb c", p=128), VS,
                       reads=VSK, writes=[("vscr", li, 2 * i), ("vscr", li, 2 * i + 1)])
            if kvonly:
                return
            qk_proj("wq", True)

            acc, acck = RA, RAK
            for c in range(8):
                pb.op("dve", lambda e: e.tensor_scalar(out=acc, in0=H[:, c, 2:1026], scalar1=PCONV[:, c * 31:c * 31 + 1],
                                                       scalar2=PCONV[:, 248 + c:249 + c], op0=ALU.mult, op1=ALU.add),
                      reads=HK + ["pconv"], writes=acck)
                for j in range(1, 30):
                    pb.op("dve", lambda e: e.scalar_tensor_tensor(out=acc, in0=H[:, c, 2 + j:1026 + j],
                                                                  scalar=PCONV[:, c * 31 + j:c * 31 + j + 1], in1=acc,
                                                                  op0=ALU.mult, op1=ALU.add),
                          reads=HK + acck, writes=acck)
                pb.op("dve", lambda e: e.scalar_tensor_tensor(out=H[:, c, 32:1056], in0=H[:, c, 32:1056],
                                                              scalar=PCONV[:, c * 31 + 30:c * 31 + 31], in1=acc,
                                                              op0=ALU.mult, op1=ALU.add),
                      reads=HK + acck, writes=HK)
            for th in range(2):
                def outc(c, t, tk, th=th):
                    pb.op("act", lambda e: e.activation(out=CT[:, c, th * 512:(th + 1) * 512], in_=t, func=AF.Silu,
                                                        scale=PCONV[:, 256 + c:257 + c], bias=PCONV[:, 264 + c:265 + c]),
                          reads=[tk, "pconv"], writes=CTK)
                ln_fm(lambda c: H[:, c, 32 + th * 512:32 + (th + 1) * 512], HK, 8, outc)

            sc = float(128.0 ** -0.5)
            for s in range(4):
                for g in range(3):
                    h = 4 * g + s
                    d = DIL[g]
                    nbk = 16 // d
                    pb.dma("sp", KH, kscr[li][:, h, :], reads=[("kscr", li, h)], writes=KHK)
                    pb.dma("sp", QH, qscr[:, h, :], reads=[("qscr", h)], writes=QHK)
                    vsrc = vscr[li][:, h * 128:(h + 1) * 128].rearrange("(kb m r) c -> m r kb c", m=128, r=d)
                    for r in range(d):
                        pb.dma("sp", VH[:, r * nbk:(r + 1) * nbk, :], vsrc[:, r, :, :], reads=[("vscr", li, h)], writes=VHK)
                    batches = []
                    if d in (1, 4):
                        nq_cls = 1024 // d // 128
                        for r in range(d):
                            for q0 in range(0, nq_cls, 2):
                                qbl = []
                                for qi in (q0, q0 + 1):
                                    mb = t0 // (128 * d) + qi
                                    kbs = []
                                    if mb >= 1:
                                        kbs.append((r * nbk + mb - 1, 2 if (qi == 0) else 1))
                                    kbs.append((r * nbk + mb, 0))
                                    qbl.append((r * (1024 // d) + qi * 128, 128, kbs))
                                if d == 1:
                                    dv = UDA[:, :, q0 * 128:q0 * 128 + 256]
                                else:
                                    dv = UDA.rearrange("p a (m r) -> p a r m", r=d)[:, :, r, :]
                                batches.append((qbl, dv))
                    else:
                        for r0 in range(0, 16, 4):
                            qbl = []
                            for r in range(r0, r0 + 4):
                                qbl.append((r * 64, 64, [(r, 0 if t0 == 0 else 3)]))
                            dv = UDA.rearrange("p a (m r) -> p a r m", r=16)[:, :, r0:r0 + 4, :]
                            batches.append((qbl, dv))
                    for bi, (qbl, dv) in enumerate(batches):
                        bsc = pb.bank()
                        col = 0
                        blocks = []
                        for (qc, nq, kbs) in qbl:
                            for (kb, mt) in kbs:
                                def fsc(pe, kb=kb, mt=mt, qc=qc, nq=nq, col=col):
                                    pe.matmul(PS[:, bsc, col:col + nq], lhsT=KH[:, kb * 128:(kb + 1) * 128], rhs=QH[:, qc:qc + nq],
                                              start=True, stop=False)
                                    return pe.matmul(PS[:, bsc, col:col + nq], lhsT=IDENT[:], rhs=NM[:, mt, 0:nq],
                                                     start=False, stop=True)
                                pb.op("pe", fsc, reads=KHK + QHK + ["ident", "nm"], writes=[("ps", bsc)])
                                blocks.append((kb, col, nq))
                                col += nq
                        pt = PT[bi % 2]
                        ptk = ("pt", bi % 2)
                        pb.op("act", lambda e: e.activation(out=pt[:, 0:col], in_=PS[:, bsc, 0:col], func=AF.Exp, scale=sc),
                              reads=[("ps", bsc)], writes=[ptk])
                        bo = pb.bank()
                        bidx = 0
                        ocol = 0
                        for (qc, nq, kbs) in qbl:
                            myb = blocks[bidx:bidx + len(kbs)]
                            bidx += len(kbs)

                            def fpv(pe, myb=myb, ocol=ocol, nq=nq):
                                ins = None
                                for i2, (kb, col2, nq2) in enumerate(myb):
                                    ins = pe.matmul(PS[:, bo, ocol:ocol + nq], lhsT=VH[:, kb, :], rhs=pt[:, col2:col2 + nq2],
                                                    start=(i2 == 0), stop=(i2 == len(myb) - 1))
                                for i2, (kb, col2, nq2) in enumerate(myb):
                                    ins = pe.matmul(PS[:, bo, 256 + ocol:256 + ocol + nq], lhsT=ONESB[:], rhs=pt[:, col2:col2 + nq2],
                                                    start=(i2 == 0), stop=(i2 == len(myb) - 1))
                                return ins
                            pb.op("pe", fpv, reads=VHK + [ptk, "onesb"], writes=[("ps", bo)])
                            ocol += nq
                        if d == 16:
                            srcv = PS[:, bo, :].rearrange("p (a r m) -> p a r m", a=2, r=4)
                        else:
                            srcv = PS[:, bo, :].rearrange("p (a m) -> p a m", a=2)
                        if g == 0:
                            pb.op("dve", lambda e: e.tensor_copy(out=dv, in_=srcv), reads=[("ps", bo)], writes=UDAK)
                        else:
                            pb.op("dve", lambda e: e.tensor_tensor(out=dv, in0=srcv, in1=dv, op=ALU.add),
                                  reads=[("ps", bo)] + UDAK, writes=UDAK)
                pb.op("dve", lambda e: e.reciprocal(out=RD, in_=UDA[:, 1, :]), reads=UDAK, writes=RDK)
                pb.op("dve", lambda e: e.tensor_tensor(out=AT[:, s, :], in0=UDA[:, 0, :], in1=RD, op=ALU.mult),
                      reads=UDAK + RDK, writes=ATK)

            for i in range(4):
                wt, wk = load_w("wgu", l * 4 + i, 4096)
                wv = wt[:, :].rearrange("p (k n) -> p k n", k=16)
                for cb in range(2):
                    for th in range(2):
                        b = pb.bank()
                        mmg(pb, PS[:, b, :], [(wv[:, kc, cb * 128:(cb + 1) * 128], XT[:, kc, th * 512:(th + 1) * 512]) for kc in range(16)],
                            reads=[wk] + XTK, wkey=("ps", b))
                        pb.op("act", lambda e: e.activation(out=G[:, 2 * i + cb, th * 512:(th + 1) * 512], in_=PS[:, b, :], func=AF.Gelu),
                              reads=[("ps", b)], writes=GK)
            for i in range(4):
                wt, wk = load_w("wgv", l * 4 + i, 4096)
                wv = wt[:, :].rearrange("p (k n) -> p k n", k=16)
                for tb in range(8):
                    b = pb.bank()
                    mmg(pb, PS[:, b, 0:256], [(XT[:, kc, tb * 128:(tb + 1) * 128], wv[:, kc, :]) for kc in range(16)],
                        reads=[wk] + XTK, wkey=("ps", b))
                    pb.op("act", lambda e: e.activation(out=VG[:, tb, i * 256:(i + 1) * 256], in_=PS[:, b, 0:256], func=AF.Gelu),
                          reads=[("ps", b)], writes=VGK)
            for g in range(8):
                pb.op("dve", lambda e: e.tensor_tensor(out=WSM[:, g, :], in0=WST[:, g * 128:(g + 1) * 128], in1=M01[:], op=ALU.mult),
                      reads=["wst", "m01"], writes=["wsm"])
            for tb in range(8):
                for hf in range(2):
                    pb.op("dve", lambda e: e.bn_stats(out=STATS[:, hf, :], in_=VG[:, tb, hf * 512:(hf + 1) * 512]),
                          reads=VGK, writes=["stats"])
                pb.op("dve", lambda e: e.bn_aggr(out=MV[:, 0:2], in_=STATS[:].rearrange("p a b -> p (a b)")),
                      reads=["stats"], writes=["mv"])
                pb.op("act", lambda e: e.activation(out=MV[:, 2:3], in_=MV[:, 1:2], func=AF.Sqrt, bias=EPSB[:], scale=1.0),
                      reads=["mv", "eps"], writes=["mv"])
                pb.op("dve", lambda e: e.reciprocal(out=MV[:, 3:4], in_=MV[:, 2:3]), reads=["mv"], writes=["mv"])
                pb.op("dve", lambda e: e.tensor_scalar(out=VG[:, tb, :], in0=VG[:, tb, :], scalar1=MV[:, 0:1], scalar2=MV[:, 3:4],
                                                       op0=ALU.subtract, op1=ALU.mult), reads=VGK + ["mv"], writes=VGK)
                pb.op("dve", lambda e: e.tensor_tensor(out=VG[:, tb, :], in0=VG[:, tb, :], in1=GB[:, 0:1024], op=ALU.mult),
                      reads=VGK + ["gb"], writes=VGK)
                pb.op("dve", lambda e: e.tensor_tensor(out=VN[:, tb, :], in0=VG[:, tb, :], in1=GB[:, 1024:2048], op=ALU.add),
                      reads=VGK + ["gb"], writes=VNK)
            for tb in range(8):
                for g0 in (0, 4):
                    b = pb.bank()
                    for gg in range(4):
                        g = g0 + gg

                        def fsp(pe, g=g, gg=gg):
                            pe.matmul(PS[:, b, gg * 128:(gg + 1) * 128], lhsT=VN[:, tb, g * 128:(g + 1) * 128], rhs=WSM[:, g, :],
                                      start=True, stop=False)
                            return pe.matmul(PS[:, b, gg * 128:(gg + 1) * 128], lhsT=ONESF[0:1, :], rhs=PAR[0:1, g * 128:(g + 1) * 128],
                                             start=False, stop=True)
                        pb.op("pe", fsp, reads=VNK + ["wsm", "onesf", "par"], writes=[("ps", b)])
                    gv = G[:, g0:g0 + 4, tb * 128:(tb + 1) * 128]
                    pb.op("dve", lambda e: e.tensor_tensor(out=gv, in0=PS[:, b, :].rearrange("p (a m) -> p a m", a=4), in1=gv, op=ALU.mult),
                          reads=[("ps", b)] + GK, writes=GK)

            for j in range(16):
                wa, wak = load_w("wmA", l * 16 + j, 2560)
                wav = wa[:, 0:2560].rearrange("p (k n) -> p k n", k=20)
                wg = []
                for t in range(3):
                    wt, wk = load_w("wmB", l * 48 + j * 3 + t, 2048)
                    wg.append((wt[:, 0:2048].rearrange("p (k n) -> p k n", k=16), wk))
                for th in range(2):
                    tsl = slice(th * 512, (th + 1) * 512)
                    by = [pb.bank() for _ in range(3)]
                    mmg(pb, PS[:, by[0], :], [(wav[:, kc, :], AT[:, kc, tsl]) for kc in range(4)], reads=[wak] + ATK, wkey=("ps", by[0]))
                    mmg(pb, PS[:, by[1], :], [(wav[:, 4 + kc, :], CT[:, kc, tsl]) for kc in range(8)], reads=[wak] + CTK, wkey=("ps", by[1]))
                    mmg(pb, PS[:, by[2], :], [(wav[:, 12 + kc, :], G[:, kc, tsl]) for kc in range(8)], reads=[wak] + GK, wkey=("ps", by[2]))
                    for t in range(3):
                        bgt = pb.bank()
                        mmg(pb, PS[:, bgt, :], [(wg[t][0][:, kc, :], XT[:, kc, tsl]) for kc in range(16)], reads=[wg[t][1]] + XTK,
                            wkey=("ps", bgt))
                        sg, sgk = tmp(t)
                        pb.op("act", lambda e: e.activation(out=sg, in_=PS[:, bgt, :], func=AF.Sigmoid), reads=[("ps", bgt)], writes=[sgk])
                        pb.op("dve", lambda e: e.tensor_tensor(out=sg, in0=PS[:, by[t], :], in1=sg, op=ALU.mult),
                              reads=[("ps", by[t]), sgk], writes=[sgk])
                    pb.op("dve", lambda e: e.tensor_tensor(out=TMP[0][:], in0=TMP[0][:], in1=TMP[1][:], op=ALU.add),
                          reads=[("tmp", 0), ("tmp", 1)], writes=[("tmp", 0)])
                    pb.op("dve", lambda e: e.tensor_tensor(out=MG[:, j, tsl], in0=TMP[0][:], in1=TMP[2][:], op=ALU.add),
                          reads=[("tmp", 0), ("tmp", 2)], writes=MGK)

            for th in range(2):
                tsl = slice(th * 512, (th + 1) * 512)
                pb.dma("sp", Z, src[:, :, t0 + th * 512:t0 + (th + 1) * 512], reads=(X1K if li > 0 else []), writes=ZK)
                for i in range(8):
                    wt, wk = load_w("wout", l * 8 + i, 4096)
                    wv = wt[:, :].rearrange("p (k n) -> p k n", k=16)
                    for cb in range(2):
                        j = 2 * i + cb
                        b = pb.bank()
                        mmg(pb, PS[:, b, :], [(wv[:, kc, cb * 128:(cb + 1) * 128], MG[:, kc, tsl]) for kc in range(16)],
                            reads=[wk] + MGK, wkey=("ps", b))
                        pb.op("dve", lambda e: e.scalar_tensor_tensor(out=Z[:, j, :], in0=Z[:, j, :], scalar=ALPHA, in1=PS[:, b, :],
                                                                      op0=ALU.mult, op1=ALU.add),
                              reads=[("ps", b)] + ZK, writes=ZK)

                def out1(c, t, tk, tsl=tsl):
                    pb.op("act", lambda e: e.activation(out=Z[:, c, :], in_=t, func=AF.Identity, scale=LNP[:, c:c + 1],
                                                        bias=LNP[:, 16 + c:17 + c]), reads=[tk, "lnp"], writes=ZK)
                    pb.op("act", lambda e: e.activation(out=XT[:, c, tsl], in_=Z[:, c, :], func=AF.Copy), reads=ZK, writes=[("xT", c)])
                ln_fm(lambda c: Z[:, c, :], ZK, 16, out1)
                pb.dma("sp", xr[:, :, tsl], Z, reads=ZK, writes=[("xr", th)])

            for i in range(44):
                wt, wk = load_w("wffn", l * 44 + i, 4096)
                wv = wt[:, :].rearrange("p (k n) -> p k n", k=16)
                for th in range(2):
                    tsl = slice(th * 512, (th + 1) * 512)
                    bg, bu = pb.bank(), pb.bank()
                    mmg(pb, PS[:, bg, :], [(wv[:, kc, 0:128], XT[:, kc, tsl]) for kc in range(16)], reads=[wk] + XTK, wkey=("ps", bg))
                    mmg(pb, PS[:, bu, :], [(wv[:, kc, 128:256], XT[:, kc, tsl]) for kc in range(16)], reads=[wk] + XTK, wkey=("ps", bu))
                    sl, slk = tmp((2 * i + th) % 4)
                    pb.op("act", lambda e: e.activation(out=sl, in_=PS[:, bg, :], func=AF.Silu), reads=[("ps", bg)], writes=[slk])
                    pb.op("dve", lambda e: e.tensor_tensor(out=HT[:, i, tsl], in0=PS[:, bu, :], in1=sl, op=ALU.mult),
                          reads=[("ps", bu), slk], writes=[("ar", i)])
            HTK = [("ar", i) for i in range(44)]
            for th in range(2):
                tsl = slice(th * 512, (th + 1) * 512)
                pb.dma("sp", Z2, xr[:, :, tsl], reads=[("xr", th)], writes=XTK)
                for j in range(16):
                    wts = [load_w("wdn", l * 32 + j * 2 + kh, 2816) for kh in range(2)]
                    b = pb.bank()
                    pairs = []
                    for kh in range(2):
                        wv = wts[kh][0][:, 0:2816].rearrange("p (k n) -> p k n", k=22)
                        for kc in range(22):
                            pairs.append((wv[:, kc, :], HT[:, kh * 22 + kc, tsl]))
                    mmg(pb, PS[:, b, :], pairs, reads=[wts[0][1], wts[1][1]] + HTK, wkey=("ps", b))
                    pb.op("dve", lambda e: e.scalar_tensor_tensor(out=Z2[:, j, :], in0=Z2[:, j, :], scalar=ALPHA, in1=PS[:, b, :],
                                                                  op0=ALU.mult, op1=ALU.add),
                          reads=[("ps", b)] + XTK, writes=XTK)

                def out2(c, t, tk):
                    pb.op("act", lambda e: e.activation(out=Z2[:, c, :], in_=t, func=AF.Identity, scale=LNP[:, 32 + c:33 + c],
                                                        bias=LNP[:, 48 + c:49 + c]), reads=[tk, "lnp"], writes=XTK)
                ln_fm(lambda c: Z2[:, c, :], XTK, 16, out2)
                pb.dma("sp", dst[:, :, tsl], Z2, reads=XTK, writes=[("dst", li, t0, th)])
                if li == 0:
                    X1K.append(("dst", li, t0, th))

        for li, lay in enumerate(layers):
            src = xin if li == 0 else x1d
            if lay == 0:
                dfull = outd if nl == 1 else x1d
                run_pass(li, 0, False, src, dfull[:, :, 0:1024])
                run_pass(li, 1024, False, src, dfull[:, :, 1024:2048])
            else:
                run_pass(li, 0, True, src, None)
                run_pass(li, 1024, False, src, outd)
        pb.finish()
    return nc


def _kt(w):
    K, N = w.shape
    return w.reshape(K // 128, 128, N).transpose(1, 0, 2)


def _tiles(w, ncol):
    K, N = w.shape
    a = _kt(w).reshape(128, K // 128, N // ncol, ncol).transpose(2, 0, 1, 3)
    return np.ascontiguousarray(a.reshape(N // ncol, 128, (K // 128) * ncol))


def _prep_layer(inp, l):
    f = lambda k: np.asarray(inp[k][l], dtype=np.float32)
    w_in = f("w_in")
    q, k, v = w_in[:, 0:1536], w_in[:, 1536:3072], w_in[:, 3072:4608]
    cv = w_in[:, 4608:6656]
    gm = w_in[:, 6656:8704]
    gt = w_in[:, 8704:14848]
    hp = (np.arange(12)[:, None] * 128 + HPERM[None, :]).reshape(-1)
    o = {}
    o["wq"] = _tiles(q[:, hp], 256)
    o["wk"] = _tiles(k[:, hp], 256)
    o["wv"] = _tiles(v, 256)
    val, gate = cv[:, 0:1024], cv[:, 1024:2048]
    cvi = np.concatenate([val.reshape(2048, 8, 128), gate.reshape(2048, 8, 128)], axis=2).reshape(2048, 2048)
    o["wconv"] = _tiles(cvi, 256)
    o["wgu"] = _tiles(gm[:, 0:1024], 256)
    o["wgv"] = _tiles(gm[:, 1024:2048], 256)
    wa = np.concatenate([f("w_attn_proj"), f("w_conv_proj"), f("w_gmlp_proj")], axis=0)
    o["wmA"] = _tiles(wa, 128)
    gts = [_tiles(gt[:, t * 2048:(t + 1) * 2048], 128) for t in range(3)]
    o["wmB"] = np.ascontiguousarray(np.stack(gts, axis=1).reshape(48, 128, 2048))
    o["wout"] = _tiles(f("w_out"), 256)
    wg, wu = f("w_ffn_gate"), f("w_ffn_up")
    gu = np.concatenate([wg.reshape(2048, 44, 128), wu.reshape(2048, 44, 128)], axis=2).reshape(2048, 44 * 256)
    o["wffn"] = _tiles(gu, 256)
    wd = f("w_ffn_down")
    dn = np.stack([_tiles(wd[kh * 2816:(kh + 1) * 2816], 128) for kh in range(2)], axis=1)
    o["wdn"] = np.ascontiguousarray(dn.reshape(32, 128, 2816))
    pc = np.zeros((128, 272), np.float32)
    pc[:, 0:248] = f("conv_dw").reshape(31, 8, 128).transpose(2, 1, 0).reshape(128, 248)
    pc[:, 248:256] = f("conv_dw_b").reshape(8, 128).T
    pc[:, 256:264] = f("conv_ln_g").reshape(8, 128).T
    pc[:, 264:272] = f("conv_ln_b").reshape(8, 128).T
    o["pconv"] = pc
    o["gbb"] = np.ascontiguousarray(np.broadcast_to(np.concatenate([f("gmlp_ln_g"), f("gmlp_ln_b")])[None, :], (128, 2048)))
    o["wst"] = np.ascontiguousarray(f("w_spatial").transpose(2, 0, 1).reshape(128, 1024))
    par = np.zeros((128, 1024), np.float32)
    par[0, :] = f("b_spatial").reshape(-1)
    o["par"] = par
    lnp = np.zeros((128, 64), np.float32)
    for i, kk in enumerate(("ln1_g", "ln1_b", "ln2_g", "ln2_b")):
        lnp[:, i * 16:(i + 1) * 16] = f(kk).reshape(16, 128).T
    o["lnp"] = lnp
    return o


def _core_consts(h):
    kk = np.arange(128)[:, None]
    qq = np.arange(128)[None, :]
    m = np.zeros((128, 4, 128), np.float32)
    m[:, 0, :] = np.where(kk <= qq, 0.0, NEG)
    m[:, 1, :] = np.where(kk >= qq, 0.0, NEG)
    m[:, 2, :] = m[:, 1, :] if h == 1 else NEG
    ok = kk <= 64 + qq
    if h == 0:
        ok = ok & (kk >= 64)
    m[:, 3, :] = np.where(ok, 0.0, NEG)
    pos = np.arange(2048, dtype=np.float32) - (0.0 if h == 1 else 1024.0)
    inv = (np.float32(ROPE_THETA) ** (-np.arange(16, dtype=np.float32) / np.float32(16))).astype(np.float32)
    ang = (pos[None, :] * inv[:, None]).astype(np.float32)
    rope = np.zeros((128, 2, 2048), np.float32)
    rope[0:16, 0] = np.cos(ang); rope[32:48, 0] = np.cos(ang)
    rope[0:16, 1] = np.sin(ang); rope[32:48, 1] = np.sin(ang)
    return {
        "masks": m.reshape(128, 512),
        "rope": rope,
        "flag": np.full((128, 1), float(h), np.float32),
        "ident": np.eye(128, dtype=np.float32),
        "m01": (kk <= qq).astype(np.float32),
    }


def _core_x(x, c):
    b, h = c // 2, c % 2
    xb = np.asarray(x[b], dtype=np.float32)
    if h == 1:
        seq = xb
    else:
        seq = np.concatenate([np.zeros((1024, D), np.float32), xb[0:1024]], axis=0)
    return np.ascontiguousarray(seq.T.reshape(16, 128, 2048).transpose(1, 0, 2))


FUSED = True
_CACHE = {}


def _get_prog(layers):
    key = tuple(layers)
    if key not in _CACHE:
        _CACHE[key] = build_program(list(layers))
    return _CACHE[key]


def kernel(**inputs):
    x = np.asarray(inputs["x"], dtype=np.float32)
    lw = [_prep_layer(inputs, l) for l in range(2)]
    cores = list(range(8))
    cc = [_core_consts(c % 2) for c in cores]
    xs = [_core_x(x, c) for c in cores]

    def wmap(ls):
        o = {}
        for k in lw[0]:
            if k in ("pconv", "gbb", "wst", "par", "lnp"):
                o[k] = np.ascontiguousarray(np.stack([lw[l][k] for l in ls], axis=0))
            else:
                o[k] = np.ascontiguousarray(np.concatenate([lw[l][k] for l in ls], axis=0))
        return o

    if FUSED:
        wm = wmap([0, 1])
        nc = _get_prog((0, 1))
        res = run_bass_kernel_spmd(nc, [dict(wm, xin=xs[c], **cc[c]) for c in cores], core_ids=cores)
        outs = [np.asarray(res.results[c]["out"]) for c in cores]
    else:
        wm = wmap([0])
        nc = _get_prog((0,))
        res = run_bass_kernel_spmd(nc, [dict(wm, xin=xs[c], **cc[c]) for c in cores], core_ids=cores)
        x1 = [np.asarray(res.results[c]["out"]) for c in cores]
        wm = wmap([1])
        nc = _get_prog((1,))
        res = run_bass_kernel_spmd(nc, [dict(wm, xin=x1[c], **cc[c]) for c in cores], core_ids=cores)
        outs = [np.asarray(res.results[c]["out"]) for c in cores]
    out = np.empty((NB, S, D), np.float32)
    for c in cores:
        b, h = c // 2, c % 2
        o = outs[c]
        out[b, h * 1024:(h + 1) * 1024, :] = o.transpose(2, 1, 0).reshape(1024, D)
    return out
```
